# Optimizing a Trainium2 kernel written in Bass

```python
import jax, jax.numpy as jnp
from jax import lax
import numpy as np

D_MODEL = 1024
BATCH = 16
SEQ = 2048
DEPTH = 1

ATTN_HEADS = 4
NOPE_DIM = 128
ROPE_DIM = 64
V_DIM = 128
QK_DIM = NOPE_DIM + ROPE_DIM
ATTN_WIDTH = ATTN_HEADS * V_DIM
Q_LORA = 384
KV_LORA = 256
POOL_GROUPS = 4
POOL_WINDOWS = (2, 4, 8, 16)
POOL_WIDTH = D_MODEL - ATTN_WIDTH
POOL_CH = POOL_WIDTH // POOL_GROUPS
MIX_WIDTH = ATTN_WIDTH + POOL_WIDTH
IN_WIDTH = Q_LORA + KV_LORA + ROPE_DIM + POOL_WIDTH
D_FF = 4 * D_MODEL
Q_BLOCK = 128
ROPE_THETA = 10000.0
EPS = 1e-6

kernel_name = "hymba_mla_multiscale_pool_block"


def rmsnorm(x, g):
    xf = x.astype(jnp.float32)
    y = xf * lax.rsqrt(jnp.mean(xf * xf, axis=-1, keepdims=True) + EPS)
    return (y * g.astype(jnp.float32)).astype(x.dtype)


def rope(x, cos, sin):
    half = x.shape[-1] // 2
    x1, x2 = x[..., :half], x[..., half:]
    xf1, xf2 = x1.astype(jnp.float32), x2.astype(jnp.float32)
    out = jnp.concatenate([xf1 * cos - xf2 * sin, xf2 * cos + xf1 * sin], axis=-1)
    return out.astype(x.dtype)


def mla(c_q, c_kv, k_r, q_norm_g, w_q_up, kv_norm_g, w_kv_up, positions):
    B, S, _ = c_q.shape
    q = (rmsnorm(c_q, q_norm_g) @ w_q_up).reshape(B, S, ATTN_HEADS, QK_DIM)
    kv = (rmsnorm(c_kv, kv_norm_g) @ w_kv_up).reshape(B, S, ATTN_HEADS, NOPE_DIM + V_DIM)
    q_nope, q_pe = q[..., :NOPE_DIM], q[..., NOPE_DIM:]
    k_nope, v = kv[..., :NOPE_DIM], kv[..., NOPE_DIM:]

    inv_freq = 1.0 / (ROPE_THETA ** (jnp.arange(0, ROPE_DIM, 2, dtype=jnp.float32) / ROPE_DIM))
    ang = positions.astype(jnp.float32)[..., None] * inv_freq
    cos, sin = jnp.cos(ang), jnp.sin(ang)
    q_pe = rope(q_pe, cos[:, :, None, :], sin[:, :, None, :])
    k_pe = rope(k_r, cos, sin)
    k_pe = jnp.broadcast_to(k_pe[:, :, None, :], (B, S, ATTN_HEADS, ROPE_DIM))

    q = jnp.concatenate([q_nope, q_pe], axis=-1).transpose(0, 2, 1, 3)
    k = jnp.concatenate([k_nope, k_pe], axis=-1).transpose(0, 2, 1, 3)
    v = v.transpose(0, 2, 1, 3)
    scale = QK_DIM ** -0.5
    nb = S // Q_BLOCK
    qb = q.reshape(B, ATTN_HEADS, nb, Q_BLOCK, QK_DIM).transpose(2, 0, 1, 3, 4)
    kpos = jnp.arange(S)

    def attend(args):
        qblk, i = args
        s = jnp.einsum('bhqd,bhkd->bhqk', qblk, k).astype(jnp.float32) * scale
        qpos = i * Q_BLOCK + jnp.arange(Q_BLOCK)
        mask = qpos[:, None] >= kpos[None, :]
        p = jax.nn.softmax(jnp.where(mask, s, -jnp.inf), axis=-1)
        return jnp.einsum('bhqk,bhkd->bhqd', p.astype(v.dtype), v)

    o = lax.map(attend, (qb, jnp.arange(nb)))
    return o.transpose(1, 0, 3, 2, 4).reshape(B, S, ATTN_WIDTH)


def multiscale_pool(u, pool_w, pool_scale):
    B, S, _ = u.shape
    ug = u.reshape(B, S, POOL_GROUPS, POOL_CH)
    t = jnp.arange(S)
    outs = []
    for g, w in enumerate(POOL_WINDOWS):
        xg = ug[:, :, g, :].astype(jnp.float32)
        cs = jnp.cumsum(xg, axis=1)
        lo = jnp.pad(cs, ((0, 0), (w, 0), (0, 0)))[:, :S]
        cnt = jnp.minimum(t + 1, w).astype(jnp.float32)[None, :, None]
        outs.append((cs - lo) / cnt - xg)
    pooled = jnp.stack(outs, axis=2).astype(u.dtype)
    y = jnp.einsum('bsgc,gcd->bsgd', pooled, pool_w).reshape(B, S, POOL_WIDTH)
    return y * pool_scale


def setup_inputs(seed: int = 0) -> dict:
    key = jax.random.key(seed)
    ks = jax.random.split(key, 20)
    f32 = jnp.float32

    def lin(k, shape, fan_in):
        return jax.random.normal(k, shape, f32) * fan_in ** -0.5

    def gain(k, shape):
        return 1.0 + 0.05 * jax.random.normal(k, shape, f32)

    x = jax.random.normal(ks[0], (BATCH, SEQ, D_MODEL), f32)
    offs = jax.random.randint(ks[1], (BATCH, 1), 0, 1024, dtype=jnp.int32)
    positions = offs + jnp.arange(SEQ, dtype=jnp.int32)[None, :]
    return {
        "x": x,
        "positions": positions,
        "norm_mix_g": gain(ks[2], (DEPTH, D_MODEL)),
        "w_in": lin(ks[3], (DEPTH, D_MODEL, IN_WIDTH), D_MODEL),
        "q_norm_g": gain(ks[4], (DEPTH, Q_LORA)),
        "w_q_up": lin(ks[5], (DEPTH, Q_LORA, ATTN_HEADS * QK_DIM), Q_LORA),
        "kv_norm_g": gain(ks[6], (DEPTH, KV_LORA)),
        "w_kv_up": lin(ks[7], (DEPTH, KV_LORA, ATTN_HEADS * (NOPE_DIM + V_DIM)), KV_LORA),
        "pool_w": lin(ks[8], (DEPTH, POOL_GROUPS, POOL_CH, POOL_CH), POOL_CH),
        "pool_scale": gain(ks[9], (DEPTH, POOL_WIDTH)),
        "w_out": lin(ks[10], (DEPTH, MIX_WIDTH, D_MODEL), MIX_WIDTH),
        "norm_mlp_g": gain(ks[11], (DEPTH, D_MODEL)),
        "w_mlp_up": lin(ks[12], (DEPTH, D_MODEL, D_FF), D_MODEL),
        "w_mlp_down": lin(ks[13], (DEPTH, D_FF, D_MODEL), D_FF),
        "norm_final_g": gain(ks[14], (D_MODEL,)),
    }


def reference(x, positions, norm_mix_g, w_in, q_norm_g, w_q_up, kv_norm_g, w_kv_up,
              pool_w, pool_scale, w_out, norm_mlp_g, w_mlp_up, w_mlp_down, norm_final_g):
    h = x
    for l in range(DEPTH):
        n = rmsnorm(h, norm_mix_g[l])
        proj = n @ w_in[l]
        c_q = proj[..., :Q_LORA]
        c_kv = proj[..., Q_LORA:Q_LORA + KV_LORA]
        k_r = proj[..., Q_LORA + KV_LORA:Q_LORA + KV_LORA + ROPE_DIM]
        u = proj[..., Q_LORA + KV_LORA + ROPE_DIM:]
        a = mla(c_q, c_kv, k_r, q_norm_g[l], w_q_up[l], kv_norm_g[l], w_kv_up[l], positions)
        p = multiscale_pool(u, pool_w[l], pool_scale[l])
        h = h + jnp.concatenate([a, p], axis=-1) @ w_out[l]
        m = rmsnorm(h, norm_mlp_g[l]) @ w_mlp_up[l]
        h = h + jnp.square(jax.nn.relu(m)) @ w_mlp_down[l]
    return rmsnorm(h, norm_final_g)
```

```python
import numpy as np
from contextlib import ExitStack
import concourse.bass as bass
import concourse.mybir as mybir
from concourse.bass_utils import run_bass_kernel_spmd

F32 = mybir.dt.float32
BF16 = mybir.dt.bfloat16
I32 = mybir.dt.int32
U8 = mybir.dt.uint8
ALU = mybir.AluOpType
AF = mybir.ActivationFunctionType

P = 128
D = 1024
SEQ = 2048
NCORES = 8
TOK = 4096
T = 512
NT = TOK // T
QL, KVL, ROPE, POOLW = 384, 256, 64, 512
INW = QL + KVL + ROPE + POOLW
DFF = 4096
EPS = 1e-6
SCALE = 192 ** -0.5
WINC = 1344
DMA_RING = 32


class Sched:
    def __init__(self):
        self.ops = []
        self.lastw = {}
        self.readers = {}
        self.last_on = {}
        self.stage = ""

    def add(self, eng, fn, reads=(), writes=(), dma=False, extra=()):
        i = len(self.ops)
        deps = set(extra)
        key = eng + ("#%d" % i if dma else "")
        for r in reads:
            w = self.lastw.get(r)
            if w is not None:
                deps.add(w)
            self.readers.setdefault(r, {})[key] = i
        for r in writes:
            w = self.lastw.get(r)
            if w is not None:
                deps.add(w)
            for q in self.readers.get(r, {}).values():
                deps.add(q)
            self.readers[r] = {}
            self.lastw[r] = i
        deps.discard(i)
        if eng == "pe" and not dma:
            deps = {d for d in deps if not (self.ops[d]["eng"] == "pe" and not self.ops[d]["dma"])}
        self.ops.append(dict(eng=eng, fn=fn, deps=deps, dma=dma, tag=self.stage))
        self.last_on[eng] = i
        return i

    def deps_of(self, names):
        ids = set()
        for r in names:
            if r in self.lastw:
                ids.add(self.lastw[r])
            ids |= set(self.readers.get(r, {}).values())
        return sorted(ids)

    def emit(self, nc, final_waits=()):
        ops = self.ops
        engs = ["pe", "act", "dve", "pool", "sp"]
        dma_list = {e: [] for e in engs}
        for i, op in enumerate(ops):
            if op["dma"]:
                lst = dma_list[op["eng"]]
                k = len(lst)
                if k >= DMA_RING:
                    op["deps"].add(lst[k - DMA_RING])
                lst.append(i)
                op["dk"] = k
        needed = set(final_waits)
        for op in ops:
            needed |= op["deps"]
        with ExitStack() as es:
            esem = {e: es.enter_context(nc.semaphore("s_" + e)) for e in engs}
            dsem = {
                e: [es.enter_context(nc.semaphore("d_%s%d" % (e, j))) for j in range(DMA_RING)]
                for e in engs if dma_list[e]
            }
            cnt = {e: 0 for e in engs}
            sig = {}
            for i, op in enumerate(ops):
                e = op["eng"]
                if op["dma"]:
                    k = op["dk"]
                    sig[i] = (dsem[e][k % DMA_RING], 16 * (k // DMA_RING + 1))
                elif i in needed:
                    cnt[e] += 1
                    sig[i] = (esem[e], cnt[e])
            block = es.enter_context(nc.Block())

            def run(e, h):
                waited = {}
                for i, op in enumerate(ops):
                    if op["eng"] != e:
                        continue
                    w = {}
                    for d in op["deps"]:
                        dop = ops[d]
                        if dop["eng"] == "pe" and e == "pe" and not dop["dma"]:
                            continue
                        s, v = sig[d]
                        key = id(s)
                        if waited.get(key, 0) >= v:
                            continue
                        if key not in w or w[key][1] < v:
                            w[key] = (s, v)
                    for key, (s, v) in w.items():
                        h.wait_ge(s, v)
                        waited[key] = v
                    ins = op["fn"](h)
                    if i in sig:
                        ins.then_inc(sig[i][0], 16 if op["dma"] else 1)
                if e == "sp":
                    for i in final_waits:
                        s, v = sig[i]
                        h.wait_ge(s, v)

            @block.tensor
            def _(h):
                run("pe", h)

            @block.scalar
            def _(h):
                run("act", h)

            @block.vector
            def _(h):
                run("dve", h)

            @block.gpsimd
            def _(h):
                run("pool", h)

            @block.sync
            def _(h):
                run("sp", h)


def build_program(nta=NT, ntb=NT):
    nc = bass.Bass("TRN2", target_bir_lowering=False)
    dt = nc.dram_tensor
    x_d = dt("x", [TOK, D], F32, kind="ExternalInput").ap()
    pos_d = dt("positions", [TOK], I32, kind="ExternalInput").ap()
    gmix_d = dt("norm_mix_g", [1, D], F32, kind="ExternalInput").ap()
    win_d = dt("w_in", [1, D, INW], F32, kind="ExternalInput").ap()
    gq_d = dt("q_norm_g", [1, QL], F32, kind="ExternalInput").ap()
    wq_d = dt("w_q_up", [1, QL, 768], F32, kind="ExternalInput").ap()
    gkv_d = dt("kv_norm_g", [1, KVL], F32, kind="ExternalInput").ap()
    wkv_d = dt("w_kv_up", [1, KVL, 1024], F32, kind="ExternalInput").ap()
    pw_d = dt("pool_w", [1, 4, P, P], F32, kind="ExternalInput").ap()
    ps_d = dt("pool_scale", [1, POOLW], F32, kind="ExternalInput").ap()
    wout_d = dt("w_out", [1, D, D], F32, kind="ExternalInput").ap()
    gmlp_d = dt("norm_mlp_g", [1, D], F32, kind="ExternalInput").ap()
    wup_d = dt("w_mlp_up", [1, D, DFF], F32, kind="ExternalInput").ap()
    wdn_d = dt("w_mlp_down", [1, DFF, D], F32, kind="ExternalInput").ap()
    gfin_d = dt("norm_final_g", [D], F32, kind="ExternalInput").ap()
    ropec_d = dt("ropec", [P, 4], F32, kind="ExternalInput").ap()
    pcorr_d = dt("poolcorr", [P, 64], F32, kind="ExternalInput").ap()
    out_d = dt("out", [TOK, D], F32, kind="ExternalOutput").ap()
    wupbf_d = dt("wup_bf", [D, DFF], BF16).ap()
    wdnbf_d = dt("wdn_bf", [DFF, D], BF16).ap()

    ARENA = 212600
    arena = nc.alloc_sbuf_tensor("arena", [P, ARENA], U8).ap()

    class Alloc:
        def __init__(self, base):
            self.off = base

        def __call__(self, shape, dtp):
            nb = int(np.prod(shape[1:])) * mybir.dt.size(dtp)
            o = self.off
            self.off = o + (nb + 63) // 64 * 64
            assert self.off <= ARENA, (self.off, ARENA)
            v = arena[0:shape[0], o:o + nb].bitcast(dtp)
            if len(shape) == 3:
                v = v.rearrange("p (a b) -> p a b", b=shape[2])
            return v

    A = Alloc(0)
    ident = A([P, P], BF16)
    identf = A([P, P], F32)
    ones = A([P, P], BF16)
    eps_t = A([P, 1], F32)
    gq_t = A([P, 3], F32)
    gkv_t = A([P, 2], F32)
    ps_t = A([P, 4], F32)
    psn_t = A([P, 4], F32)
    ropec = A([P, 4], F32)
    pcorr = A([P, 4, 16], F32)
    ss_t = A([P, 8], F32)
    rs_t = A([P, 8], F32)
    xt = [A([P, 4, D], F32) for _ in range(2)]
    xn = [A([P, D], BF16) for _ in range(2)]
    xnT = A([P, 8, T], BF16)
    common_end = A.off

    A = Alloc(common_end)
    gmix_b = A([P, D], F32)
    win = A([P, 8, WINC], BF16)
    wq = A([P, 3, 1280], BF16)
    wkv = A([P, 2, 1024], BF16)
    craw = A([P, 5, T], F32)
    sqcn = A([P, 5, T], BF16)
    rstd_b = A([P, 2, T], F32)
    early_dead_end = A.off
    poolw = A([P, 4, P], BF16)
    wout = A([P, 8, D], BF16)
    knT = A([P, 4, SEQ], BF16)
    kpe_lo = A([P, SEQ], BF16)
    kpe_hi = A([P, SEQ], BF16)
    Vc = A([P, 16, 512], BF16)
    ub = A([P, 4, T + 16], F32)
    sA = A([P, T + 16], F32)
    sB = A([P, T + 16], F32)
    pooled = A([P, 4, T], BF16)
    qn = A([P, 4, T], BF16)
    qpe = A([P, 2, T], BF16)
    posi = A([P, T], I32)
    rA = A([P, T], F32)
    rB = A([P, T], F32)
    cosT = A([P, T], F32)
    sinT = A([P, T], F32)
    t1, t2 = rA, rB
    pT = [A([P, T], BF16) for _ in range(4)]
    rec = A([P, T], F32)
    mixT = A([P, 8, T], BF16)
    maskB = A([P, P], BF16)
    sel = A([P, P], BF16)
    phaseA_end = A.off

    A = Alloc(common_end)
    gmlp_b = A([P, D], F32)
    gfin_b = A([P, D], F32)
    wupA = A([P, 8, 2048], BF16)
    assert A.off <= early_dead_end, (A.off, early_dead_end)
    wupB = A([P, 8, 2048], BF16)
    wupH = [wupA, wupB]
    wdn = A([P, 32, D], BF16)
    aT = A([P, 16, T], BF16)
    rr = [A([P, T], F32) for _ in range(2)]
    xnB = [A([P, D], BF16) for _ in range(2)]
    phaseB_end = A.off

    pm = [nc.alloc_psum_tensor("pm%d" % i, [P, 512], F32).ap() for i in range(8)]
    pmb = [b.bitcast(BF16) for b in pm]

    S = Sched()
    free_banks = list(range(8))

    def acquire():
        return free_banks.pop(0)

    def release(k):
        free_banks.append(k)

    def mm(out, lhsT, rhs, start, stop, reads, writes):
        return S.add("pe", lambda e: e.matmul(out, lhsT=lhsT, rhs=rhs, start=start, stop=stop), reads, writes)

    def tr(out, in_, reads, writes):
        return S.add("pe", lambda e: e.transpose(out=out, in_=in_, identity=ident), list(reads) + ["ident"], writes)

    def act(out, in_, func, reads, writes, **kw):
        return S.add("act", lambda e: e.activation(out=out, in_=in_, func=func, **kw), reads, writes)

    def actmul(out, in_, m, reads, writes):
        return S.add("act", lambda e: e.mul(out, in_, m), reads, writes)

    def tt(eng, out, in0, in1, op, reads, writes):
        return S.add(eng, lambda e: e.tensor_tensor(out=out, in0=in0, in1=in1, op=op), reads, writes)

    def stt(eng, out, in0, scalar, in1, op0, op1, reads, writes):
        return S.add(eng, lambda e: e.scalar_tensor_tensor(out=out, in0=in0, scalar=scalar, in1=in1, op0=op0, op1=op1), reads, writes)

    def ts(eng, out, in0, s1, s2, op0, op1, reads, writes):
        return S.add(eng, lambda e: e.tensor_scalar(out=out, in0=in0, scalar1=s1, scalar2=s2, op0=op0, op1=op1), reads, writes)

    def tss(eng, out, in_, s, op, reads, writes):
        return S.add(eng, lambda e: e.tensor_single_scalar(out=out, in_=in_, scalar=s, op=op), reads, writes)

    def cp(eng, out, in_, reads, writes):
        return S.add(eng, lambda e: e.tensor_copy(out=out, in_=in_), reads, writes)

    def ms(eng, ap, val, writes):
        return S.add(eng, lambda e: e.memset(ap, val), (), writes)

    def dma(eng, out, in_, reads, writes, extra=()):
        return S.add(eng, lambda e: e.dma_start(out=out, in_=in_), reads, writes, dma=True, extra=extra)

    def col(ap1d, lo, n):
        return ap1d[lo:lo + n].rearrange("(p o) -> p o", o=1)

    ms("pool", identf, 0.0, ["identf"])
    S.add("pool", lambda e: e.affine_select(out=identf, in_=identf, pattern=[[1, P]], compare_op=ALU.not_equal,
                                            fill=1.0, base=0, channel_multiplier=-1), ["identf"], ["identf"])
    cp("pool", ident, identf, ["identf"], ["ident"])
    ms("pool", ones, 1.0, ["ones"])
    cp("pool", sel, ident, ["ident"], ["sel"])
    cp("pool", sel[0:64, 64:128], ident[0:64, 0:64], ["ident"], ["sel"])
    cp("pool", sel[64:128, 0:64], ident[64:128, 64:128], ["ident"], ["sel"])
    ms("pool", maskB, -30000.0, ["maskB"])
    S.add("pool", lambda e: e.affine_select(out=maskB, in_=maskB, pattern=[[-1, P]], compare_op=ALU.is_gt,
                                            fill=0.0, base=0, channel_multiplier=1), ["maskB"], ["maskB"])
    ms("pool", eps_t, EPS, ["eps"])
    for b_ in range(4):
        x0_dma = dma("sp", xt[0][:, b_, :], x_d[b_ * P:(b_ + 1) * P, :], (), [("xt", 0, b_)])
    dma("sp", gmix_b, gmix_d[0].partition_broadcast(P), (), ["gmix"])
    dma("sp", posi, pos_d[0:T].partition_broadcast(P), (), ["posi"])
    dma("sp", ropec, ropec_d, (), ["ropec"])
    dma("sp", pcorr, pcorr_d.rearrange("p (g n) -> p g n", g=4), (), ["pcorr"])
    for c in range(3):
        dma("sp", gq_t[:, c:c + 1], col(gq_d[0], c * P, P), (), [("gq", c)])
    for c in range(2):
        dma("sp", gkv_t[:, c:c + 1], col(gkv_d[0], c * P, P), (), [("gkv", c)])
    for c in range(4):
        dma("sp", ps_t[:, c:c + 1], col(ps_d[0], c * P, P), (), [("ps", c)])

    winv = win_d[0].rearrange("(c p) n -> p c n", p=P)
    dma("pool", win[:, :, 0:640], winv[:, :, 0:640], (), ["win_a"])
    dma("pool", win[:, :, 640:INW], winv[:, :, 640:INW], (), ["win"], extra=[x0_dma])
    dma("pool", wq[:, :, 0:768], wq_d[0].rearrange("(c p) n -> p c n", p=P), (), ["wq"])
    dma("pool", wkv, wkv_d[0].rearrange("(c p) n -> p c n", p=P), (), ["wkv"])
    woutv = wout_d[0].rearrange("(c p) n -> p c n", p=P)

    X1_DMAS = []

    def late_weight_dmas():
        dma("pool", poolw, pw_d[0].rearrange("g c d -> c g d"), (), ["poolw"])
        for c4 in range(2):
            dma("pool", wout[:, c4 * 4:(c4 + 1) * 4, :], woutv[:, c4 * 4:(c4 + 1) * 4, :], (), ["wout"], extra=list(X1_DMAS))

    STAGE = []
    for q4 in range(4):
        STAGE.append((wupbf_d[:, q4 * 1024:(q4 + 1) * 1024], wup_d[0][:, q4 * 1024:(q4 + 1) * 1024], ("wupbf", q4)))
    for q4 in range(4):
        STAGE.append((wdnbf_d[q4 * 1024:(q4 + 1) * 1024, :], wdn_d[0][q4 * 1024:(q4 + 1) * 1024, :], ("wdnbf", q4)))

    def staging_dmas(n):
        for _ in range(n):
            if STAGE:
                o, i, r = STAGE.pop(0)
                dma("pool", o, i, (), [r])

    def blk_rows(dram, t, b):
        return dram[t * T + b * P:t * T + (b + 1) * P, :]

    def tile_rows(dram, t):
        return dram[t * T:(t + 1) * T, :].rearrange("(b p) n -> p b n", p=P)

    XNR = [xn]

    def norm_pre(slot, b, gb, gname, so, extra=()):
        xs = xt[slot]
        ring = XNR[0]
        xb, xr = ring[b % len(ring)], ("xn", b % len(ring))
        S.add("act", lambda e: e.activation(out=xb, in_=xs[:, b, :], func=AF.Square, accum_out=ss_t[:, so + b:so + b + 1]),
              [("xt", slot, b)], [xr, ("ss", so + b)], extra=extra)
        act(rs_t[:, so + b:so + b + 1], ss_t[:, so + b:so + b + 1], AF.Ln, [("ss", so + b), "eps"], [("rs", so + b)], scale=1.0 / D, bias=eps_t)
        act(rs_t[:, so + b:so + b + 1], rs_t[:, so + b:so + b + 1], AF.Exp, [("rs", so + b)], [("rs", so + b)], scale=-0.5)
        S.add("dve", lambda e: e.scalar_tensor_tensor(out=xb, in0=xs[:, b, :], scalar=rs_t[:, so + b:so + b + 1], in1=gb,
                                                      op0=ALU.mult, op1=ALU.mult),
              [("xt", slot, b), ("rs", so + b), gname], [xr], extra=extra)

    def norm_tr(b, extra=()):
        ring = XNR[0]
        xb, xr = ring[b % len(ring)], ("xn", b % len(ring))
        k = acquire()
        for c in range(8):
            S.add("pe", lambda e, c=c: e.transpose(out=pmb[k][:, c * P:(c + 1) * P], in_=xb[:, c * P:(c + 1) * P], identity=ident),
                  [xr, "ident"], [("pm", k)], extra=extra)
        src = pmb[k][:, 0:1024].rearrange("p (c n) -> p c n", c=8)
        cp("dve", xnT[:, :, b * P:(b + 1) * P], src, [("pm", k)], [("xnT", b)])
        release(k)

    def norm_all(slot, gb, gname, extra=()):
        norm_pre(slot, 0, gb, gname, 0, extra)
        norm_pre(slot, 1, gb, gname, 0, extra)
        norm_tr(0, extra)
        norm_pre(slot, 2, gb, gname, 0, extra)
        norm_tr(1, extra)
        norm_pre(slot, 3, gb, gname, 0, extra)
        norm_tr(2, extra)
        norm_tr(3, extra)

    XNT_ALL = [("xnT", b) for b in range(4)]

    def XT(slot):
        return [("xt", slot, b) for b in range(4)]

    KPE_ALL = [("kpeT", cc) for cc in range(4)]
    ts("pool", psn_t, ps_t, -1.0, None, ALU.mult, ALU.bypass, [("ps", c_) for c_ in range(4)], ["psn"])
    ms("pool", kpe_lo, 0.0, KPE_ALL)
    ms("pool", kpe_hi, 0.0, KPE_ALL)
    def rope_tables(t):
        S.stage = "A%d:ropetab" % t
        if t > 0:
            dma("sp", posi, pos_d[t * T:(t + 1) * T].partition_broadcast(P), (), ["posi"])
        cp("dve", rA, posi, ["posi"], ["rA"])
        ts("dve", rA, rA, ropec[:, 0:1], 0.5, ALU.mult, ALU.add, ["rA", "ropec"], ["rA"])
        cp("dve", posi, rA, ["rA"], ["posi"])
        cp("dve", rB, posi, ["posi"], ["rB"])
        tt("dve", rA, rA, rB, ALU.subtract, ["rA", "rB"], ["rA"])
        tss("dve", rB, rA, 0.0, ALU.is_lt, ["rA"], ["rB"])
        tt("dve", rA, rA, rB, ALU.add, ["rA", "rB"], ["rA"])
        act(sinT, rA, AF.Sin, ["rA", "ropec"], ["sinT"], scale=ropec[:, 1:2], bias=ropec[:, 2:3])
        ts("dve", rB, rA, 0.25, None, ALU.add, ALU.bypass, ["rA"], ["rB"])
        tss("dve", rA, rB, 1.0, ALU.is_ge, ["rB"], ["rA"])
        tt("dve", rB, rB, rA, ALU.subtract, ["rB", "rA"], ["rB"])
        act(cosT, rB, AF.Sin, ["rB", "ropec"], ["cosT"], scale=float(2 * np.pi), bias=ropec[:, 3:4])

    pslot = [0]
    S.stage = "A0:norm"
    norm_all(0, gmix_b, "gmix")
    rope_tables(0)
    S.stage = "A0:wcopy"
    for (d0, s0, n) in [(1216, 640, 64), (1280, 672, 32), (1312, 640, 32)]:
        cp("pool", win[:, :, d0:d0 + n], win[:, :, s0:s0 + n], ["win"], ["win"])
    wq_nat = wq[:, :, 0:768].rearrange("p c (h d) -> p c h d", d=192)
    wq_p = wq[:, :, 768:1024].rearrange("p c (h d) -> p c h d", d=64)
    wq_s = wq[:, :, 1024:1280].rearrange("p c (h d) -> p c h d", d=64)
    cp("pool", wq_p, wq_nat[:, :, :, 128:192], ["wq"], ["wq"])
    cp("pool", wq_s[:, :, :, 0:32], wq_nat[:, :, :, 160:192], ["wq"], ["wq"])
    cp("pool", wq_s[:, :, :, 32:64], wq_nat[:, :, :, 128:160], ["wq"], ["wq"])
    for t in range(nta):
        s, c = t // 4, t % 4
        slot = t % 2
        if t + 1 < nta:
            for b_ in range(4):
                i_ = dma("sp", xt[1 - slot][:, b_, :], blk_rows(x_d, t + 1, b_), (), [("xt", 1 - slot, b_)])
                if t == 0:
                    X1_DMAS.append(i_)
        S.stage = "A%d:win" % t

        def win_group(c0):
            k = acquire()
            wn = "win_a" if c0 < 640 else "win"
            for kc in range(8):
                mm(pm[k], win[:, kc, c0:c0 + P], xnT[:, kc, :], kc == 0, kc == 7, [wn] + XNT_ALL, [("pm", k)])
            return k

        def c_group(j):
            k = win_group(j * P)
            cp("dve", craw[:, j, :], pm[k], [("pm", k)], [("craw", j)])
            release(k)
            act(sqcn[:, j, :], craw[:, j, :], AF.Square, [("craw", j)], [("sqcn", j)])

        def stats(which, j0, nj):
            k = acquire()
            for j in range(j0, j0 + nj):
                mm(pm[k], ones, sqcn[:, j, :], j == j0, j == j0 + nj - 1, ["ones", ("sqcn", j)], [("pm", k)])
            act(rstd_b[:, which, :], pm[k], AF.Ln, [("pm", k), "eps"], [("rstd", which)], scale=1.0 / (nj * P), bias=eps_t)
            release(k)
            act(rstd_b[:, which, :], rstd_b[:, which, :], AF.Exp, [("rstd", which)], [("rstd", which)], scale=-0.5)
            for j in range(j0, j0 + nj):
                gcol = gq_t[:, j:j + 1] if j < 3 else gkv_t[:, j - 3:j - 2]
                stt("dve", sqcn[:, j, :], craw[:, j, :], gcol, rstd_b[:, which, :], ALU.mult, ALU.mult,
                    [("craw", j)] + [("gq", c_) for c_ in range(3)] + [("gkv", c_) for c_ in range(2)] + [("rstd", which)], [("sqcn", j)])

        c_group(0)
        c_group(1)
        c_group(2)
        c_group(3)
        c_group(4)
        stats(0, 0, 3)
        if t + 1 < nta and t > 0:
            S.stage = "A%d:norm" % (t + 1)
            norm_pre(1 - slot, 0, gmix_b, "gmix", 0)
            norm_pre(1 - slot, 1, gmix_b, "gmix", 0)
            S.stage = "A%d:win" % t
        km = win_group(1216)
        ktmp = pooled[:, 0, :]
        tt("dve", ktmp[0:64, :], pm[km][0:64, :], cosT[0:64, :], ALU.mult, [("pm", km), "cosT"], [("pooled", 0)])
        tt("dve", ktmp[64:128, :], pm[km][64:128, :], sinT[64:128, :], ALU.mult, [("pm", km), "sinT"], [("pooled", 0)])
        release(km)
        if t == 0:
            late_weight_dmas()
        stats(1, 3, 2)
        if c == 0:
            ms("pool", ub[:, :, 0:16], 0.0, [("ub", g) for g in range(4)])
        for g in range(4):
            k = win_group(704 + g * P)
            act(ub[:, g, 16:16 + T], pm[k], AF.Copy, [("pm", k)], [("ub", g)])
            release(k)

        nx = t + 1 < nta and t > 0
        UPS = "A%d:up" % t
        NXS = "A%d:norm" % (t + 1)
        S.stage = UPS
        for pr in range(2):
            kp = acquire()
            for j in range(3):
                mm(pm[kp], wq[:, j, 768 + pr * P:768 + (pr + 1) * P], sqcn[:, j, :], j == 0, j == 2, ["wq", ("sqcn", j)], [("pm", kp)])
            ks = acquire()
            for j in range(3):
                mm(pm[ks], wq[:, j, 1024 + pr * P:1024 + (pr + 1) * P], sqcn[:, j, :], j == 0, j == 2, ["wq", ("sqcn", j)], [("pm", ks)])
            tt("dve", t1, pm[kp], cosT, ALU.mult, [("pm", kp), "cosT"], ["rA"])
            release(kp)
            tt("dve", t2, pm[ks], sinT, ALU.mult, [("pm", ks), "sinT"], ["rB"])
            release(ks)
            tt("dve", qpe[:, pr, :], t1, t2, ALU.add, ["rA", "rB"], [("qpe", pr)])
        if nx:
            S.stage = NXS
            norm_tr(0)
            norm_pre(1 - slot, 2, gmix_b, "gmix", 0)
        S.stage = UPS
        for h in range(4):
            k = acquire()
            for j in range(3):
                mm(pm[k], wq[:, j, h * 192:h * 192 + P], sqcn[:, j, :], j == 0, j == 2, ["wq", ("sqcn", j)], [("pm", k)])
            if h < 2:
                act(qn[:, h, :], pm[k], AF.Copy, [("pm", k)], [("qn", h)])
            else:
                cp("dve", qn[:, h, :], pm[k], [("pm", k)], [("qn", h)])
            release(k)
        if nx:
            S.stage = NXS
            norm_tr(1)
            norm_pre(1 - slot, 3, gmix_b, "gmix", 0)
        S.stage = UPS
        for h in range(4):
            k = acquire()
            for j in range(2):
                mm(pm[k], wkv[:, j, h * 256:h * 256 + P], sqcn[:, 3 + j, :], j == 0, j == 1, ["wkv", ("sqcn", 3 + j)], [("pm", k)])
            act(knT[:, h, c * T:(c + 1) * T], pm[k], AF.Copy, [("pm", k)], [("knT", h, c)])
            release(k)
        if nx:
            S.stage = NXS
            norm_tr(2)
        S.stage = UPS
        for b in range(4):
            k = acquire()
            for j in range(2):
                mm(pm[k], sqcn[:, 3 + j, b * P:(b + 1) * P], wkv[:, j, :].rearrange("p (h d) -> p h d", d=256)[:, :, 128:256], j == 0, j == 1, ["wkv", ("sqcn", 3 + j)], [("pm", k)])
            act(Vc[:, 4 * c + b, :], pm[k], AF.Copy, [("pm", k)], [("V", 4 * c + b)])
            release(k)
        if nx:
            S.stage = NXS
            norm_tr(3)

        S.stage = UPS
        k2 = acquire()
        mm(pm[k2], sel, pooled[:, 0, :], True, True, ["sel", ("pooled", 0)], [("pm", k2)])
        cp("dve", kpe_lo[0:64, c * T:(c + 1) * T], pm[k2][0:64, :], [("pm", k2)], [("kpeT", c)])
        cp("dve", kpe_hi[64:128, c * T:(c + 1) * T], pm[k2][64:128, :], [("pm", k2)], [("kpeT", c)])
        release(k2)
        last = (t == nta - 1) and ntb > 0
        if last:
            S.stage = "B:early"
            early = S.deps_of(["gmix", "win", "win_a", "wq", "wkv"] + [("craw", j) for j in range(5)]
                              + [("sqcn", j) for j in range(5)] + [("rstd", w_) for w_ in range(2)])
            dma("sp", gmlp_b, gmlp_d[0].partition_broadcast(P), (), ["gmlp"], extra=early)
            dma("sp", gfin_b, gfin_d.partition_broadcast(P), (), ["gfin"], extra=early)
            for b_ in range(4):
                dma("sp", xt[0][:, b_, :], blk_rows(out_d, 0, b_), [("outd", 0, b_)], [("xt", 0, b_)])
            wupv = wupbf_d.rearrange("(c p) n -> p c n", p=P)
            for kc2 in range(4):
                dma("sp", wupA[:, kc2 * 2:kc2 * 2 + 2, :], wupv[:, kc2 * 2:kc2 * 2 + 2, 0:2048],
                    [("wupbf", 0), ("wupbf", 1)], [("wup", 0)], extra=early)

        def pool_group(g):
            w = 2 << g
            L = g + 1
            src = ub[:, g, :]
            srcn = ("ub", g)
            for l in range(1, L + 1):
                st = 16 - w + (1 << l)
                hh_ = 1 << (l - 1)
                dst, dstn = (sA, "sA") if l % 2 == 1 else (sB, "sB")
                tt("dve", dst[:, st:T + 16], src[:, st:T + 16], src[:, st - hh_:T + 16 - hh_], ALU.add, [srcn], [dstn])
                src, srcn = dst, dstn
            if c == 0:
                tt("dve", src[:, 16:16 + w - 1], src[:, 16:16 + w - 1], pcorr[:, g, 0:w - 1], ALU.mult, [srcn, "pcorr"], [srcn])
            stt("dve", pooled[:, g, :], src[:, 16:16 + T], -1.0 / w, ub[:, g, 16:16 + T], ALU.mult, ALU.add,
                [srcn, ("ub", g)], [("pooled", g)])

        if c == 0:
            S.stage = "A%d:pool" % t
            for g_ in (3, 2, 1, 0):
                pool_group(g_)
        nkb = 4 * c + 4
        LOOK = 4
        AB = [acquire() for _ in range(8)]
        pend_fin = [None]
        sring = [0]
        for h in range(4):
            S.stage = "A%d:attn" % t
            pr, hh = h // 2, h % 2
            kpe_h = kpe_lo if hh == 0 else kpe_hi
            kO, kR = AB[4 + 2 * (h % 2)], AB[5 + 2 * (h % 2)]
            sbk = {}

            def issue_S(j):
                q0 = max(0, j - 4 * c) * P
                nq = T - q0
                diag = j >= 4 * c
                k = AB[sring[0] % 4]
                sring[0] += 1
                mm(pm[k][:, 0:nq], knT[:, h, j * P:(j + 1) * P], qn[:, h, q0:T], True, False,
                   [("knT", h, j // 4), ("qn", h)], [("pm", k)])
                mm(pm[k][:, 0:nq], kpe_h[:, j * P:(j + 1) * P], qpe[:, pr, q0:T], False, not diag,
                   [("kpeT", j // 4), ("qpe", pr)], [("pm", k)])
                if diag:
                    mm(pm[k][:, 0:P], ident, maskB, False, True, ["ident", "maskB"], [("pm", k)])
                sbk[j] = k

            for j in range(min(LOOK, nkb)):
                issue_S(j)
            for j in range(nkb):
                jj = j - 4 * c
                q0 = max(0, jj) * P
                nq = T - q0
                k = sbk.pop(j)
                sl = pslot[0]
                pslot[0] = (sl + 1) % 4
                act(pT[sl][:, 0:nq], pm[k][:, 0:nq], AF.Exp, [("pm", k)], [("pT", sl)], scale=SCALE)
                if j + LOOK < nkb:
                    issue_S(j + LOOK)
                mm(pm[kO][:, q0:T], Vc[:, j, h * P:(h + 1) * P], pT[sl][:, 0:nq], j == 0, j == nkb - 1,
                   [("V", j), ("pT", sl)], [("pm", kO)])
                mm(pm[kR][:, q0:T], ones, pT[sl][:, 0:nq], j == 0, j == nkb - 1, ["ones", ("pT", sl)], [("pm", kR)])
                if j == 1 and pend_fin[0] is not None:
                    pend_fin[0]()
                    pend_fin[0] = None

            def fin(h=h, kO=kO, kR=kR):
                act(rec, pm[kR], AF.Ln, [("pm", kR)], ["rec"])
                act(rec, rec, AF.Exp, ["rec"], ["rec"], scale=-1.0)
                tt("dve", mixT[:, h, :], pm[kO], rec, ALU.mult, [("pm", kO), "rec"], [("mixT", h)])

            if h == 3:
                fin()
            else:
                pend_fin[0] = fin
            S.stage = "A%d:pool" % t
            if h == 0 and c > 0:
                pool_group(3)
            elif h == 1:
                if c > 0:
                    pool_group(2)
                    pool_group(1)
                    pool_group(0)
                if c < 3:
                    cp("dve", ub[:, :, 0:16], ub[:, :, T:T + 16], [("ub", g) for g in range(4)], [("ub", g) for g in range(4)])

        for k_ in (AB[0], AB[1], AB[2], AB[3], AB[4], AB[5], AB[6], AB[7]):
            release(k_)
        if last:
            S.stage = "B0:norm"
            norm_all(0, gmlp_b, "gmlp")
        t0n = (t == 0 and nta > 1)
        if t0n:
            S.stage = "A1:norm"
            norm_pre(1, 0, gmix_b, "gmix", 0)
            norm_pre(1, 1, gmix_b, "gmix", 0)
        S.stage = "A%d:pool" % t
        for g in range(4):
            k = acquire()
            mm(pm[k], poolw[:, g, :], pooled[:, g, :], True, True, ["poolw", ("pooled", g)], [("pm", k)])
            actmul(mixT[:, 4 + g, :], pm[k], psn_t[:, g:g + 1], [("pm", k), "psn"], [("mixT", 4 + g)])
            release(k)

        if t + 1 < nta:
            rope_tables(t + 1)
        staging_dmas({1: 1, 2: 2, 3: 2, 4: 2, 5: 1}.get(t, 0) if nta == NT else (2 if t < 2 else 1))
        S.stage = "A%d:wout" % t

        def wout_groups(bs):
            ks = {g_: acquire() for g_ in bs}
            for part in ((0, 1, 2), (4, 5, 6, 7), (3,)):
                for (b, nh) in bs:
                    for kc in part:
                        mm(pm[ks[(b, nh)]], mixT[:, kc, b * P:(b + 1) * P], wout[:, kc, nh * 512:(nh + 1) * 512], kc == 0, kc == 3,
                           [("mixT", kc), "wout"], [("pm", ks[(b, nh)])])
            for (b, nh) in bs:
                dst = xt[slot][:, b, nh * 512:(nh + 1) * 512]
                tt("dve", dst, pm[ks[(b, nh)]], dst, ALU.add, [("pm", ks[(b, nh)]), ("xt", slot, b)], [("xt", slot, b)])
                release(ks[(b, nh)])
            for b in sorted({b_ for b_, _ in bs}):
                dma("sp", blk_rows(out_d, t, b), xt[slot][:, b, :], [("xt", slot, b)], [("outd", t, b)])

        if t0n:
            for b in range(4):
                if b >= 1:
                    S.stage = "A1:norm"
                    norm_tr(b - 1)
                    if b + 1 < 4:
                        norm_pre(1, b + 1, gmix_b, "gmix", 0)
                    S.stage = "A%d:wout" % t
                wout_groups([(b, 0), (b, 1)])
        else:
            wout_groups([(0, 0), (0, 1), (1, 0), (1, 1)])
            wout_groups([(2, 0), (2, 1)])
            wout_groups([(3, 0), (3, 1)])
        if t0n:
            S.stage = "A1:norm"
            norm_tr(3)

    S.stage = "B:load"
    barrier = [S.last_on[e] for e in ["pe", "act", "dve", "pool", "sp"]]
    wupv = wupbf_d.rearrange("(c p) n -> p c n", p=P)
    wdnv = wdnbf_d.rearrange("(c p) n -> p c n", p=P)

    def load_wdn(g4):
        dma("sp", wdn[:, g4 * 4:(g4 + 1) * 4, :], wdnv[:, g4 * 4:(g4 + 1) * 4, :], [("wdnbf", g4 // 2)], [("wdn", g4)], extra=barrier)

    for g4 in range(4):
        load_wdn(g4)
    for kc2 in range(4):
        dma("sp", wupB[:, kc2 * 2:kc2 * 2 + 2, :], wupv[:, kc2 * 2:kc2 * 2 + 2, 2048:4096],
            [("wupbf", 2), ("wupbf", 3)], [("wup", 1)], extra=barrier)
    for g4 in range(4, 8):
        load_wdn(g4)

    finals = []
    XNR[0] = [xn[0], xn[1], xnB[0], xnB[1]]

    def up_half(t, half):
        S.stage = "B%d:up%d" % (t, half)
        ex = barrier if (t == 0 and half == 0) else ()
        for f16 in range(16):
            k = acquire()
            for kc in range(8):
                mm(pm[k], wupH[half][:, kc, f16 * P:(f16 + 1) * P], xnT[:, kc, :], kc == 0, kc == 7, [("wup", half)] + XNT_ALL, [("pm", k)])
            r, rn = rr[f16 % 2], ("rr", f16 % 2)
            S.add("act", lambda e, r=r, k=k: e.activation(out=r, in_=pm[k], func=AF.Relu), [("pm", k)], [rn], extra=ex)
            release(k)
            S.add("dve" if f16 % 2 == 0 else "pool", lambda e, r=r, f16=f16: e.tensor_tensor(out=aT[:, f16, :], in0=r, in1=r, op=ALU.mult),
                  [rn], [("aT", f16)], extra=ex)

    def fin_block(t, b):
        slot = t % 2
        xs = xt[slot][:, b, :]
        S.add("act", lambda e: e.activation(out=rr[b % 2][:, 0:512].bitcast(BF16), in_=xs, func=AF.Square, accum_out=ss_t[:, 4 + b:5 + b]),
              [("xt", slot, b)], [("rr", b % 2), ("ss", 4 + b)])
        act(rs_t[:, 4 + b:5 + b], ss_t[:, 4 + b:5 + b], AF.Ln, [("ss", 4 + b), "eps"], [("rs", 4 + b)], scale=1.0 / D, bias=eps_t)
        act(rs_t[:, 4 + b:5 + b], rs_t[:, 4 + b:5 + b], AF.Exp, [("rs", 4 + b)], [("rs", 4 + b)], scale=-0.5)
        stt("dve", xs, xs, rs_t[:, 4 + b:5 + b], gfin_b, ALU.mult, ALU.mult, [("xt", slot, b), ("rs", 4 + b), "gfin"], [("xt", slot, b)])
        finals.append(dma("sp", blk_rows(out_d, t, b), xs, [("xt", slot, b)], [("outd", t, b)]))

    def down_half(t, half, fin=False):
        slot = t % 2
        for b in range(4):
            S.stage = "B%d:down%d" % (t, half)
            for nh in range(2):
                k = acquire()
                for f16 in range(16):
                    f = half * 16 + f16
                    mm(pm[k], aT[:, f16, b * P:(b + 1) * P], wdn[:, f, nh * 512:(nh + 1) * 512], f16 == 0, f16 == 15,
                       [("aT", f16), ("wdn", f // 4)], [("pm", k)])
                dst = xt[slot][:, b, nh * 512:(nh + 1) * 512]
                tt("dve", dst, pm[k], dst, ALU.add, [("pm", k), ("xt", slot, b)], [("xt", slot, b)])
                release(k)
            if fin:
                S.stage = "B%d:fin" % t
                fin_block(t, b)

    for t in range(ntb):
        slot = t % 2
        ring_guard = barrier if t == 0 else ()
        if t + 1 < ntb:
            for b_ in range(4):
                dma("sp", xt[1 - slot][:, b_, :], blk_rows(out_d, t + 1, b_), [("outd", t + 1, b_)], [("xt", 1 - slot, b_)])
        up_half(t, 0)
        down_half(t, 0)
        nxt = t + 1 < ntb
        if nxt:
            S.stage = "B%d:norm" % (t + 1)
            for b_ in range(4):
                norm_pre(1 - slot, b_, gmlp_b, "gmlp", 0, extra=ring_guard)
        up_half(t, 1)
        if nxt:
            S.stage = "B%d:norm" % (t + 1)
            for b_ in range(4):
                norm_tr(b_)
        down_half(t, 1, fin=True)
    S.emit(nc, final_waits=finals)
    return nc


_CACHE = {}


def _consts():
    inv = (np.float32(1.0) / np.power(np.float32(10000.0), np.arange(0, ROPE, 2, dtype=np.float32) / np.float32(ROPE))).astype(np.float32)
    ropec = np.zeros((P, 4), np.float32)
    for p in range(P):
        sgn = -1.0 if (p % 64) < 32 else 1.0
        ropec[p] = (inv[p % 32] / (2 * np.pi), sgn * 2 * np.pi, -sgn * np.pi, -np.pi)
    pc = np.ones((P, 4, 16), np.float32)
    for g in range(4):
        w = 2 << g
        for tt_ in range(w - 1):
            pc[:, g, tt_] = w / (tt_ + 1.0)
    return ropec, pc.reshape(P, 64)


def kernel(**inputs):
    x = np.asarray(inputs["x"], np.float32)
    pos = np.asarray(inputs["positions"], np.int32)
    if "nc" not in _CACHE:
        _CACHE["nc"] = build_program()
    nc = _CACHE["nc"]
    ropec, pc = _consts()
    wnames = ["norm_mix_g", "w_in", "q_norm_g", "w_q_up", "kv_norm_g", "w_kv_up", "pool_w", "pool_scale",
              "w_out", "norm_mlp_g", "w_mlp_up", "w_mlp_down", "norm_final_g"]
    shared = {n: np.ascontiguousarray(np.asarray(inputs[n], np.float32)) for n in wnames}
    in_maps = []
    for i in range(NCORES):
        m = dict(shared)
        m["x"] = np.ascontiguousarray(x[2 * i:2 * i + 2].reshape(TOK, D))
        m["positions"] = np.ascontiguousarray(pos[2 * i:2 * i + 2].reshape(TOK))
        m["ropec"] = ropec
        m["poolcorr"] = pc
        in_maps.append(m)
    res = run_bass_kernel_spmd(nc, in_maps, core_ids=list(range(NCORES)))
    out = np.concatenate([np.asarray(r["out"]).reshape(2, SEQ, D) for r in res.results], axis=0)
    return out.astype(np.float32)
```

```python
import numpy as np
from contextlib import ExitStack
import concourse.bass as bass
import concourse.mybir as mybir
from concourse.bass_utils import run_bass_kernel_spmd

F32 = mybir.dt.float32
BF16 = mybir.dt.bfloat16
I32 = mybir.dt.int32
U8 = mybir.dt.uint8
ALU = mybir.AluOpType
AF = mybir.ActivationFunctionType

P = 128
D = 1024
SEQ = 2048
NCORES = 8
TOK = 4096
T = 512
NT = TOK // T
QL, KVL, ROPE, POOLW = 384, 256, 64, 512
INW = QL + KVL + ROPE + POOLW
DFF = 4096
EPS = 1e-6
SCALE = 192 ** -0.5
WINC = 1344
DMA_RING = 32


class Sched:
    def __init__(self):
        self.ops = []
        self.lastw = {}
        self.readers = {}
        self.last_on = {}
        self.stage = ""

    def add(self, eng, fn, reads=(), writes=(), dma=False, extra=()):
        i = len(self.ops)
        deps = set(extra)
        key = eng + ("#%d" % i if dma else "")
        for r in reads:
            w = self.lastw.get(r)
            if w is not None:
                deps.add(w)
            self.readers.setdefault(r, {})[key] = i
        for r in writes:
            w = self.lastw.get(r)
            if w is not None:
                deps.add(w)
            for q in self.readers.get(r, {}).values():
                deps.add(q)
            self.readers[r] = {}
            self.lastw[r] = i
        deps.discard(i)
        if eng == "pe" and not dma:
            deps = {d for d in deps if not (self.ops[d]["eng"] == "pe" and not self.ops[d]["dma"])}
        self.ops.append(dict(eng=eng, fn=fn, deps=deps, dma=dma, tag=self.stage))
        self.last_on[eng] = i
        return i

    def deps_of(self, names):
        ids = set()
        for r in names:
            if r in self.lastw:
                ids.add(self.lastw[r])
            ids |= set(self.readers.get(r, {}).values())
        return sorted(ids)

    def emit(self, nc, final_waits=()):
        ops = self.ops
        engs = ["pe", "act", "dve", "pool", "sp"]
        dma_list = {e: [] for e in engs}
        for i, op in enumerate(ops):
            if op["dma"]:
                lst = dma_list[op["eng"]]
                k = len(lst)
                if k >= DMA_RING:
                    op["deps"].add(lst[k - DMA_RING])
                lst.append(i)
                op["dk"] = k
        needed = set(final_waits)
        for op in ops:
            needed |= op["deps"]
        with ExitStack() as es:
            esem = {e: es.enter_context(nc.semaphore("s_" + e)) for e in engs}
            dsem = {
                e: [es.enter_context(nc.semaphore("d_%s%d" % (e, j))) for j in range(DMA_RING)]
                for e in engs if dma_list[e]
            }
            cnt = {e: 0 for e in engs}
            sig = {}
            for i, op in enumerate(ops):
                e = op["eng"]
                if op["dma"]:
                    k = op["dk"]
                    sig[i] = (dsem[e][k % DMA_RING], 16 * (k // DMA_RING + 1))
                elif i in needed:
                    cnt[e] += 1
                    sig[i] = (esem[e], cnt[e])
            block = es.enter_context(nc.Block())

            def run(e, h):
                waited = {}
                for i, op in enumerate(ops):
                    if op["eng"] != e:
                        continue
                    w = {}
                    for d in op["deps"]:
                        dop = ops[d]
                        if dop["eng"] == "pe" and e == "pe" and not dop["dma"]:
                            continue
                        s, v = sig[d]
                        key = id(s)
                        if waited.get(key, 0) >= v:
                            continue
                        if key not in w or w[key][1] < v:
                            w[key] = (s, v)
                    for key, (s, v) in w.items():
                        h.wait_ge(s, v)
                        waited[key] = v
                    ins = op["fn"](h)
                    if i in sig:
                        ins.then_inc(sig[i][0], 16 if op["dma"] else 1)
                if e == "sp":
                    for i in final_waits:
                        s, v = sig[i]
                        h.wait_ge(s, v)

            @block.tensor
            def _(h):
                run("pe", h)

            @block.scalar
            def _(h):
                run("act", h)

            @block.vector
            def _(h):
                run("dve", h)

            @block.gpsimd
            def _(h):
                run("pool", h)

            @block.sync
            def _(h):
                run("sp", h)


def build_program(nta=NT, ntb=NT):
    nc = bass.Bass("TRN2", target_bir_lowering=False)
    dt = nc.dram_tensor
    x_d = dt("x", [TOK, D], F32, kind="ExternalInput").ap()
    pos_d = dt("positions", [TOK], I32, kind="ExternalInput").ap()
    gmix_d = dt("norm_mix_g", [1, D], F32, kind="ExternalInput").ap()
    win_d = dt("w_in", [1, D, INW], F32, kind="ExternalInput").ap()
    gq_d = dt("q_norm_g", [1, QL], F32, kind="ExternalInput").ap()
    wq_d = dt("w_q_up", [1, QL, 768], F32, kind="ExternalInput").ap()
    gkv_d = dt("kv_norm_g", [1, KVL], F32, kind="ExternalInput").ap()
    wkv_d = dt("w_kv_up", [1, KVL, 1024], F32, kind="ExternalInput").ap()
    pw_d = dt("pool_w", [1, 4, P, P], F32, kind="ExternalInput").ap()
    ps_d = dt("pool_scale", [1, POOLW], F32, kind="ExternalInput").ap()
    wout_d = dt("w_out", [1, D, D], F32, kind="ExternalInput").ap()
    gmlp_d = dt("norm_mlp_g", [1, D], F32, kind="ExternalInput").ap()
    wup_d = dt("w_mlp_up", [1, D, DFF], F32, kind="ExternalInput").ap()
    wdn_d = dt("w_mlp_down", [1, DFF, D], F32, kind="ExternalInput").ap()
    gfin_d = dt("norm_final_g", [D], F32, kind="ExternalInput").ap()
    ropec_d = dt("ropec", [P, 4], F32, kind="ExternalInput").ap()
    pcorr_d = dt("poolcorr", [P, 64], F32, kind="ExternalInput").ap()
    out_d = dt("out", [TOK, D], F32, kind="ExternalOutput").ap()
    wupbf_d = dt("wup_bf", [D, DFF], BF16).ap()
    wdnbf_d = dt("wdn_bf", [DFF, D], BF16).ap()

    ARENA = 212600
    arena = nc.alloc_sbuf_tensor("arena", [P, ARENA], U8).ap()

    class Alloc:
        def __init__(self, base):
            self.off = base

        def __call__(self, shape, dtp):
            nb = int(np.prod(shape[1:])) * mybir.dt.size(dtp)
            o = self.off
            self.off = o + (nb + 63) // 64 * 64
            assert self.off <= ARENA, (self.off, ARENA)
            v = arena[0:shape[0], o:o + nb].bitcast(dtp)
            if len(shape) == 3:
                v = v.rearrange("p (a b) -> p a b", b=shape[2])
            return v

    A = Alloc(0)
    ident = A([P, P], BF16)
    identf = A([P, P], F32)
    ones = A([P, P], BF16)
    eps_t = A([P, 1], F32)
    gq_t = A([P, 3], F32)
    gkv_t = A([P, 2], F32)
    ps_t = A([P, 4], F32)
    psn_t = A([P, 4], F32)
    ropec = A([P, 4], F32)
    pcorr = A([P, 4, 16], F32)
    ss_t = A([P, 8], F32)
    rs_t = A([P, 8], F32)
    xt = [A([P, 4, D], F32) for _ in range(2)]
    xn = [A([P, D], BF16) for _ in range(2)]
    xnT = A([P, 8, T], BF16)
    common_end = A.off

    A = Alloc(common_end)
    gmix_b = A([P, D], F32)
    win = A([P, 8, WINC], BF16)
    wq = A([P, 3, 1280], BF16)
    wkv = A([P, 2, 1024], BF16)
    craw = A([P, 5, T], F32)
    sqcn = A([P, 5, T], BF16)
    rstd_b = A([P, 2, T], F32)
    early_dead_end = A.off
    poolw = A([P, 4, P], BF16)
    wout = A([P, 8, D], BF16)
    knT = A([P, 4, SEQ], BF16)
    kpe_lo = A([P, SEQ], BF16)
    kpe_hi = A([P, SEQ], BF16)
    Vc = A([P, 16, 512], BF16)
    ub = A([P, 4, T + 16], F32)
    sA = A([P, T + 16], F32)
    sB = A([P, T + 16], F32)
    pooled = A([P, 4, T], BF16)
    qn = A([P, 4, T], BF16)
    qpe = A([P, 2, T], BF16)
    posi = A([P, T], I32)
    rA = A([P, T], F32)
    rB = A([P, T], F32)
    cosT = A([P, T], F32)
    sinT = A([P, T], F32)
    t1, t2 = rA, rB
    pT = [A([P, T], BF16) for _ in range(4)]
    rec = A([P, T], F32)
    mixT = A([P, 8, T], BF16)
    maskB = A([P, P], BF16)
    sel = A([P, P], BF16)
    phaseA_end = A.off

    A = Alloc(common_end)
    gmlp_b = A([P, D], F32)
    gfin_b = A([P, D], F32)
    wupA = A([P, 8, 2048], BF16)
    assert A.off <= early_dead_end, (A.off, early_dead_end)
    wupB = A([P, 8, 2048], BF16)
    wupH = [wupA, wupB]
    wdn = A([P, 32, D], BF16)
    aT = A([P, 16, T], BF16)
    rr = [A([P, T], F32) for _ in range(2)]
    xnB = [A([P, D], BF16) for _ in range(2)]
    phaseB_end = A.off

    pm = [nc.alloc_psum_tensor("pm%d" % i, [P, 512], F32).ap() for i in range(8)]
    pmb = [b.bitcast(BF16) for b in pm]

    S = Sched()
    free_banks = list(range(8))

    def acquire():
        return free_banks.pop(0)

    def release(k):
        free_banks.append(k)

    def mm(out, lhsT, rhs, start, stop, reads, writes):
        return S.add("pe", lambda e: e.matmul(out, lhsT=lhsT, rhs=rhs, start=start, stop=stop), reads, writes)

    def tr(out, in_, reads, writes):
        return S.add("pe", lambda e: e.transpose(out=out, in_=in_, identity=ident), list(reads) + ["ident"], writes)

    def act(out, in_, func, reads, writes, **kw):
        return S.add("act", lambda e: e.activation(out=out, in_=in_, func=func, **kw), reads, writes)

    def actmul(out, in_, m, reads, writes):
        return S.add("act", lambda e: e.mul(out, in_, m), reads, writes)

    def tt(eng, out, in0, in1, op, reads, writes):
        return S.add(eng, lambda e: e.tensor_tensor(out=out, in0=in0, in1=in1, op=op), reads, writes)

    def stt(eng, out, in0, scalar, in1, op0, op1, reads, writes):
        return S.add(eng, lambda e: e.scalar_tensor_tensor(out=out, in0=in0, scalar=scalar, in1=in1, op0=op0, op1=op1), reads, writes)

    def ts(eng, out, in0, s1, s2, op0, op1, reads, writes):
        return S.add(eng, lambda e: e.tensor_scalar(out=out, in0=in0, scalar1=s1, scalar2=s2, op0=op0, op1=op1), reads, writes)

    def tss(eng, out, in_, s, op, reads, writes):
        return S.add(eng, lambda e: e.tensor_single_scalar(out=out, in_=in_, scalar=s, op=op), reads, writes)

    def cp(eng, out, in_, reads, writes):
        return S.add(eng, lambda e: e.tensor_copy(out=out, in_=in_), reads, writes)

    def ms(eng, ap, val, writes):
        return S.add(eng, lambda e: e.memset(ap, val), (), writes)

    def dma(eng, out, in_, reads, writes, extra=()):
        return S.add(eng, lambda e: e.dma_start(out=out, in_=in_), reads, writes, dma=True, extra=extra)

    def col(ap1d, lo, n):
        return ap1d[lo:lo + n].rearrange("(p o) -> p o", o=1)

    ms("pool", identf, 0.0, ["identf"])
    S.add("pool", lambda e: e.affine_select(out=identf, in_=identf, pattern=[[1, P]], compare_op=ALU.not_equal,
                                            fill=1.0, base=0, channel_multiplier=-1), ["identf"], ["identf"])
    cp("pool", ident, identf, ["identf"], ["ident"])
    ms("pool", ones, 1.0, ["ones"])
    cp("pool", sel, ident, ["ident"], ["sel"])
    cp("pool", sel[0:64, 64:128], ident[0:64, 0:64], ["ident"], ["sel"])
    cp("pool", sel[64:128, 0:64], ident[64:128, 64:128], ["ident"], ["sel"])
    ms("pool", maskB, -30000.0, ["maskB"])
    S.add("pool", lambda e: e.affine_select(out=maskB, in_=maskB, pattern=[[-1, P]], compare_op=ALU.is_gt,
                                            fill=0.0, base=0, channel_multiplier=1), ["maskB"], ["maskB"])
    ms("pool", eps_t, EPS, ["eps"])
    for b_ in range(4):
        x0_dma = dma("sp", xt[0][:, b_, :], x_d[b_ * P:(b_ + 1) * P, :], (), [("xt", 0, b_)])
    dma("sp", gmix_b, gmix_d[0].partition_broadcast(P), (), ["gmix"])
    dma("sp", posi, pos_d[0:T].partition_broadcast(P), (), ["posi"])
    dma("sp", ropec, ropec_d, (), ["ropec"])
    dma("sp", pcorr, pcorr_d.rearrange("p (g n) -> p g n", g=4), (), ["pcorr"])
    for c in range(3):
        dma("sp", gq_t[:, c:c + 1], col(gq_d[0], c * P, P), (), [("gq", c)])
    for c in range(2):
        dma("sp", gkv_t[:, c:c + 1], col(gkv_d[0], c * P, P), (), [("gkv", c)])
    for c in range(4):
        dma("sp", ps_t[:, c:c + 1], col(ps_d[0], c * P, P), (), [("ps", c)])

    winv = win_d[0].rearrange("(c p) n -> p c n", p=P)
    dma("pool", win[:, :, 0:640], winv[:, :, 0:640], (), ["win_a"])
    dma("pool", win[:, :, 640:INW], winv[:, :, 640:INW], (), ["win"], extra=[x0_dma])
    dma("pool", wq[:, :, 0:768], wq_d[0].rearrange("(c p) n -> p c n", p=P), (), ["wq"])
    dma("pool", wkv, wkv_d[0].rearrange("(c p) n -> p c n", p=P), (), ["wkv"])
    woutv = wout_d[0].rearrange("(c p) n -> p c n", p=P)

    X1_DMAS = []

    def late_weight_dmas():
        dma("pool", poolw, pw_d[0].rearrange("g c d -> c g d"), (), ["poolw"])
        for c4 in range(2):
            dma("pool", wout[:, c4 * 4:(c4 + 1) * 4, :], woutv[:, c4 * 4:(c4 + 1) * 4, :], (), ["wout"], extra=list(X1_DMAS))

    STAGE = []
    for q4 in range(4):
        STAGE.append((wupbf_d[:, q4 * 1024:(q4 + 1) * 1024], wup_d[0][:, q4 * 1024:(q4 + 1) * 1024], ("wupbf", q4)))
    for q4 in range(4):
        STAGE.append((wdnbf_d[q4 * 1024:(q4 + 1) * 1024, :], wdn_d[0][q4 * 1024:(q4 + 1) * 1024, :], ("wdnbf", q4)))

    def staging_dmas(n):
        for _ in range(n):
            if STAGE:
                o, i, r = STAGE.pop(0)
                dma("pool", o, i, (), [r])

    def blk_rows(dram, t, b):
        return dram[t * T + b * P:t * T + (b + 1) * P, :]

    def tile_rows(dram, t):
        return dram[t * T:(t + 1) * T, :].rearrange("(b p) n -> p b n", p=P)

    XNR = [xn]

    def norm_pre(slot, b, gb, gname, so, extra=()):
        xs = xt[slot]
        ring = XNR[0]
        xb, xr = ring[b % len(ring)], ("xn", b % len(ring))
        S.add("act", lambda e: e.activation(out=xb, in_=xs[:, b, :], func=AF.Square, accum_out=ss_t[:, so + b:so + b + 1]),
              [("xt", slot, b)], [xr, ("ss", so + b)], extra=extra)
        act(rs_t[:, so + b:so + b + 1], ss_t[:, so + b:so + b + 1], AF.Ln, [("ss", so + b), "eps"], [("rs", so + b)], scale=1.0 / D, bias=eps_t)
        act(rs_t[:, so + b:so + b + 1], rs_t[:, so + b:so + b + 1], AF.Exp, [("rs", so + b)], [("rs", so + b)], scale=-0.5)
        S.add("dve", lambda e: e.scalar_tensor_tensor(out=xb, in0=xs[:, b, :], scalar=rs_t[:, so + b:so + b + 1], in1=gb,
                                                      op0=ALU.mult, op1=ALU.mult),
              [("xt", slot, b), ("rs", so + b), gname], [xr], extra=extra)

    def norm_tr(b, extra=()):
        ring = XNR[0]
        xb, xr = ring[b % len(ring)], ("xn", b % len(ring))
        k = acquire()
        for c in range(8):
            S.add("pe", lambda e, c=c: e.transpose(out=pmb[k][:, c * P:(c + 1) * P], in_=xb[:, c * P:(c + 1) * P], identity=ident),
                  [xr, "ident"], [("pm", k)], extra=extra)
        src = pmb[k][:, 0:1024].rearrange("p (c n) -> p c n", c=8)
        cp("dve", xnT[:, :, b * P:(b + 1) * P], src, [("pm", k)], [("xnT", b)])
        release(k)

    def norm_all(slot, gb, gname, extra=()):
        norm_pre(slot, 0, gb, gname, 0, extra)
        norm_pre(slot, 1, gb, gname, 0, extra)
        norm_tr(0, extra)
        norm_pre(slot, 2, gb, gname, 0, extra)
        norm_tr(1, extra)
        norm_pre(slot, 3, gb, gname, 0, extra)
        norm_tr(2, extra)
        norm_tr(3, extra)

    XNT_ALL = [("xnT", b) for b in range(4)]

    def XT(slot):
        return [("xt", slot, b) for b in range(4)]

    KPE_ALL = [("kpeT", cc) for cc in range(4)]
    ts("pool", psn_t, ps_t, -1.0, None, ALU.mult, ALU.bypass, [("ps", c_) for c_ in range(4)], ["psn"])
    ms("pool", kpe_lo, 0.0, KPE_ALL)
    ms("pool", kpe_hi, 0.0, KPE_ALL)
    def rope_tables(t):
        S.stage = "A%d:ropetab" % t
        if t > 0:
            dma("sp", posi, pos_d[t * T:(t + 1) * T].partition_broadcast(P), (), ["posi"])
        cp("dve", rA, posi, ["posi"], ["rA"])
        ts("dve", rA, rA, ropec[:, 0:1], 0.5, ALU.mult, ALU.add, ["rA", "ropec"], ["rA"])
        cp("dve", posi, rA, ["rA"], ["posi"])
        cp("dve", rB, posi, ["posi"], ["rB"])
        tt("dve", rA, rA, rB, ALU.subtract, ["rA", "rB"], ["rA"])
        tss("dve", rB, rA, 0.0, ALU.is_lt, ["rA"], ["rB"])
        tt("dve", rA, rA, rB, ALU.add, ["rA", "rB"], ["rA"])
        act(sinT, rA, AF.Sin, ["rA", "ropec"], ["sinT"], scale=ropec[:, 1:2], bias=ropec[:, 2:3])
        ts("dve", rB, rA, 0.25, None, ALU.add, ALU.bypass, ["rA"], ["rB"])
        tss("dve", rA, rB, 1.0, ALU.is_ge, ["rB"], ["rA"])
        tt("dve", rB, rB, rA, ALU.subtract, ["rB", "rA"], ["rB"])
        act(cosT, rB, AF.Sin, ["rB", "ropec"], ["cosT"], scale=float(2 * np.pi), bias=ropec[:, 3:4])

    pslot = [0]
    S.stage = "A0:norm"
    norm_all(0, gmix_b, "gmix")
    rope_tables(0)
    S.stage = "A0:wcopy"
    for (d0, s0, n) in [(1216, 640, 64), (1280, 672, 32), (1312, 640, 32)]:
        cp("pool", win[:, :, d0:d0 + n], win[:, :, s0:s0 + n], ["win"], ["win"])
    wq_nat = wq[:, :, 0:768].rearrange("p c (h d) -> p c h d", d=192)
    wq_p = wq[:, :, 768:1024].rearrange("p c (h d) -> p c h d", d=64)
    wq_s = wq[:, :, 1024:1280].rearrange("p c (h d) -> p c h d", d=64)
    cp("pool", wq_p, wq_nat[:, :, :, 128:192], ["wq"], ["wq"])
    cp("pool", wq_s[:, :, :, 0:32], wq_nat[:, :, :, 160:192], ["wq"], ["wq"])
    cp("pool", wq_s[:, :, :, 32:64], wq_nat[:, :, :, 128:160], ["wq"], ["wq"])
    for t in range(nta):
        s, c = t // 4, t % 4
        slot = t % 2
        if t + 1 < nta:
            for b_ in range(4):
                i_ = dma("sp", xt[1 - slot][:, b_, :], blk_rows(x_d, t + 1, b_), (), [("xt", 1 - slot, b_)])
                if t == 0:
                    X1_DMAS.append(i_)
        S.stage = "A%d:win" % t

        def win_group(c0):
            k = acquire()
            wn = "win_a" if c0 < 640 else "win"
            for kc in range(8):
                mm(pm[k], win[:, kc, c0:c0 + P], xnT[:, kc, :], kc == 0, kc == 7, [wn] + XNT_ALL, [("pm", k)])
            return k

        def c_group(j):
            k = win_group(j * P)
            cp("dve", craw[:, j, :], pm[k], [("pm", k)], [("craw", j)])
            release(k)
            act(sqcn[:, j, :], craw[:, j, :], AF.Square, [("craw", j)], [("sqcn", j)])

        def stats(which, j0, nj):
            k = acquire()
            for j in range(j0, j0 + nj):
                mm(pm[k], ones, sqcn[:, j, :], j == j0, j == j0 + nj - 1, ["ones", ("sqcn", j)], [("pm", k)])
            act(rstd_b[:, which, :], pm[k], AF.Ln, [("pm", k), "eps"], [("rstd", which)], scale=1.0 / (nj * P), bias=eps_t)
            release(k)
            act(rstd_b[:, which, :], rstd_b[:, which, :], AF.Exp, [("rstd", which)], [("rstd", which)], scale=-0.5)
            for j in range(j0, j0 + nj):
                gcol = gq_t[:, j:j + 1] if j < 3 else gkv_t[:, j - 3:j - 2]
                stt("dve", sqcn[:, j, :], craw[:, j, :], gcol, rstd_b[:, which, :], ALU.mult, ALU.mult,
                    [("craw", j)] + [("gq", c_) for c_ in range(3)] + [("gkv", c_) for c_ in range(2)] + [("rstd", which)], [("sqcn", j)])

        c_group(0)
        c_group(1)
        c_group(2)
        c_group(3)
        c_group(4)
        stats(0, 0, 3)
        if t + 1 < nta and t > 0:
            S.stage = "A%d:norm" % (t + 1)
            norm_pre(1 - slot, 0, gmix_b, "gmix", 0)
            norm_pre(1 - slot, 1, gmix_b, "gmix", 0)
            S.stage = "A%d:win" % t
        km = win_group(1216)
        ktmp = pooled[:, 0, :]
        tt("dve", ktmp[0:64, :], pm[km][0:64, :], cosT[0:64, :], ALU.mult, [("pm", km), "cosT"], [("pooled", 0)])
        tt("dve", ktmp[64:128, :], pm[km][64:128, :], sinT[64:128, :], ALU.mult, [("pm", km), "sinT"], [("pooled", 0)])
        release(km)
        if t == 0:
            late_weight_dmas()
        stats(1, 3, 2)
        if c == 0:
            ms("pool", ub[:, :, 0:16], 0.0, [("ub", g) for g in range(4)])
        for g in range(4):
            k = win_group(704 + g * P)
            act(ub[:, g, 16:16 + T], pm[k], AF.Copy, [("pm", k)], [("ub", g)])
            release(k)

        nx = t + 1 < nta and t > 0
        UPS = "A%d:up" % t
        NXS = "A%d:norm" % (t + 1)
        S.stage = UPS
        for pr in range(2):
            kp = acquire()
            for j in range(3):
                mm(pm[kp], wq[:, j, 768 + pr * P:768 + (pr + 1) * P], sqcn[:, j, :], j == 0, j == 2, ["wq", ("sqcn", j)], [("pm", kp)])
            ks = acquire()
            for j in range(3):
                mm(pm[ks], wq[:, j, 1024 + pr * P:1024 + (pr + 1) * P], sqcn[:, j, :], j == 0, j == 2, ["wq", ("sqcn", j)], [("pm", ks)])
            tt("dve", t1, pm[kp], cosT, ALU.mult, [("pm", kp), "cosT"], ["rA"])
            release(kp)
            tt("dve", t2, pm[ks], sinT, ALU.mult, [("pm", ks), "sinT"], ["rB"])
            release(ks)
            tt("dve", qpe[:, pr, :], t1, t2, ALU.add, ["rA", "rB"], [("qpe", pr)])
        if nx:
            S.stage = NXS
            norm_tr(0)
            norm_pre(1 - slot, 2, gmix_b, "gmix", 0)
        S.stage = UPS
        for h in range(4):
            k = acquire()
            for j in range(3):
                mm(pm[k], wq[:, j, h * 192:h * 192 + P], sqcn[:, j, :], j == 0, j == 2, ["wq", ("sqcn", j)], [("pm", k)])
            if h < 2:
                act(qn[:, h, :], pm[k], AF.Copy, [("pm", k)], [("qn", h)])
            else:
                cp("dve", qn[:, h, :], pm[k], [("pm", k)], [("qn", h)])
            release(k)
        if nx:
            S.stage = NXS
            norm_tr(1)
            norm_pre(1 - slot, 3, gmix_b, "gmix", 0)
        S.stage = UPS
        for h in range(4):
            k = acquire()
            for j in range(2):
                mm(pm[k], wkv[:, j, h * 256:h * 256 + P], sqcn[:, 3 + j, :], j == 0, j == 1, ["wkv", ("sqcn", 3 + j)], [("pm", k)])
            act(knT[:, h, c * T:(c + 1) * T], pm[k], AF.Copy, [("pm", k)], [("knT", h, c)])
            release(k)
        if nx:
            S.stage = NXS
            norm_tr(2)
        S.stage = UPS
        for b in range(4):
            k = acquire()
            for j in range(2):
                mm(pm[k], sqcn[:, 3 + j, b * P:(b + 1) * P], wkv[:, j, :].rearrange("p (h d) -> p h d", d=256)[:, :, 128:256], j == 0, j == 1, ["wkv", ("sqcn", 3 + j)], [("pm", k)])
            act(Vc[:, 4 * c + b, :], pm[k], AF.Copy, [("pm", k)], [("V", 4 * c + b)])
            release(k)
        if nx:
            S.stage = NXS
            norm_tr(3)

        S.stage = UPS
        k2 = acquire()
        mm(pm[k2], sel, pooled[:, 0, :], True, True, ["sel", ("pooled", 0)], [("pm", k2)])
        cp("dve", kpe_lo[0:64, c * T:(c + 1) * T], pm[k2][0:64, :], [("pm", k2)], [("kpeT", c)])
        cp("dve", kpe_hi[64:128, c * T:(c + 1) * T], pm[k2][64:128, :], [("pm", k2)], [("kpeT", c)])
        release(k2)
        last = (t == nta - 1) and ntb > 0
        if last:
            S.stage = "B:early"
            early = S.deps_of(["gmix", "win", "win_a", "wq", "wkv"] + [("craw", j) for j in range(5)]
                              + [("sqcn", j) for j in range(5)] + [("rstd", w_) for w_ in range(2)])
            dma("sp", gmlp_b, gmlp_d[0].partition_broadcast(P), (), ["gmlp"], extra=early)
            dma("sp", gfin_b, gfin_d.partition_broadcast(P), (), ["gfin"], extra=early)
            for b_ in range(4):
                dma("sp", xt[0][:, b_, :], blk_rows(out_d, 0, b_), [("outd", 0, b_)], [("xt", 0, b_)])
            wupv = wupbf_d.rearrange("(c p) n -> p c n", p=P)
            for kc2 in range(4):
                dma("sp", wupA[:, kc2 * 2:kc2 * 2 + 2, :], wupv[:, kc2 * 2:kc2 * 2 + 2, 0:2048],
                    [("wupbf", 0), ("wupbf", 1)], [("wup", 0)], extra=early)

        def pool_group(g):
            w = 2 << g
            L = g + 1
            src = ub[:, g, :]
            srcn = ("ub", g)
            for l in range(1, L + 1):
                st = 16 - w + (1 << l)
                hh_ = 1 << (l - 1)
                dst, dstn = (sA, "sA") if l % 2 == 1 else (sB, "sB")
                tt("dve", dst[:, st:T + 16], src[:, st:T + 16], src[:, st - hh_:T + 16 - hh_], ALU.add, [srcn], [dstn])
                src, srcn = dst, dstn
            if c == 0:
                tt("dve", src[:, 16:16 + w - 1], src[:, 16:16 + w - 1], pcorr[:, g, 0:w - 1], ALU.mult, [srcn, "pcorr"], [srcn])
            stt("dve", pooled[:, g, :], src[:, 16:16 + T], -1.0 / w, ub[:, g, 16:16 + T], ALU.mult, ALU.add,
                [srcn, ("ub", g)], [("pooled", g)])

        if c == 0:
            S.stage = "A%d:pool" % t
            for g_ in (3, 2, 1, 0):
                pool_group(g_)
        nkb = 4 * c + 4
        LOOK = 4
        AB = [acquire() for _ in range(8)]
        pend_fin = [None]
        sring = [0]
        for h in range(4):
            S.stage = "A%d:attn" % t
            pr, hh = h // 2, h % 2
            kpe_h = kpe_lo if hh == 0 else kpe_hi
            kO, kR = AB[4 + 2 * (h % 2)], AB[5 + 2 * (h % 2)]
            sbk = {}

            def issue_S(j):
                q0 = max(0, j - 4 * c) * P
                nq = T - q0
                diag = j >= 4 * c
                k = AB[sring[0] % 4]
                sring[0] += 1
                mm(pm[k][:, 0:nq], knT[:, h, j * P:(j + 1) * P], qn[:, h, q0:T], True, False,
                   [("knT", h, j // 4), ("qn", h)], [("pm", k)])
                mm(pm[k][:, 0:nq], kpe_h[:, j * P:(j + 1) * P], qpe[:, pr, q0:T], False, not diag,
                   [("kpeT", j // 4), ("qpe", pr)], [("pm", k)])
                if diag:
                    mm(pm[k][:, 0:P], ident, maskB, False, True, ["ident", "maskB"], [("pm", k)])
                sbk[j] = k

            for j in range(min(LOOK, nkb)):
                issue_S(j)
            for j in range(nkb):
                jj = j - 4 * c
                q0 = max(0, jj) * P
                nq = T - q0
                k = sbk.pop(j)
                sl = pslot[0]
                pslot[0] = (sl + 1) % 4
                act(pT[sl][:, 0:nq], pm[k][:, 0:nq], AF.Exp, [("pm", k)], [("pT", sl)], scale=SCALE)
                if j + LOOK < nkb:
                    issue_S(j + LOOK)
                mm(pm[kO][:, q0:T], Vc[:, j, h * P:(h + 1) * P], pT[sl][:, 0:nq], j == 0, j == nkb - 1,
                   [("V", j), ("pT", sl)], [("pm", kO)])
                mm(pm[kR][:, q0:T], ones, pT[sl][:, 0:nq], j == 0, j == nkb - 1, ["ones", ("pT", sl)], [("pm", kR)])
                if j == 1 and pend_fin[0] is not None:
                    pend_fin[0]()
                    pend_fin[0] = None

            def fin(h=h, kO=kO, kR=kR):
                act(rec, pm[kR], AF.Ln, [("pm", kR)], ["rec"])
                act(rec, rec, AF.Exp, ["rec"], ["rec"], scale=-1.0)
                tt("dve", mixT[:, h, :], pm[kO], rec, ALU.mult, [("pm", kO), "rec"], [("mixT", h)])

            if h == 3 and t == nta - 1:
                fin()
            else:
                pend_fin[0] = fin
            S.stage = "A%d:pool" % t
            if h == 0 and c > 0:
                pool_group(3)
            elif h == 1:
                if c > 0:
                    pool_group(2)
                    pool_group(1)
                    pool_group(0)
                if c < 3:
                    cp("dve", ub[:, :, 0:16], ub[:, :, T:T + 16], [("ub", g) for g in range(4)], [("ub", g) for g in range(4)])

        for k_ in (AB[0], AB[1], AB[2], AB[3], AB[4], AB[5], AB[6], AB[7]):
            release(k_)
        if last:
            S.stage = "B0:norm"
            norm_all(0, gmlp_b, "gmlp")
        t0n = (t == 0 and nta > 1)
        if t0n:
            S.stage = "A1:norm"
            norm_pre(1, 0, gmix_b, "gmix", 0)
            norm_pre(1, 1, gmix_b, "gmix", 0)
        S.stage = "A%d:pool" % t
        for g in range(4):
            k = acquire()
            mm(pm[k], poolw[:, g, :], pooled[:, g, :], True, True, ["poolw", ("pooled", g)], [("pm", k)])
            actmul(mixT[:, 4 + g, :], pm[k], psn_t[:, g:g + 1], [("pm", k), "psn"], [("mixT", 4 + g)])
            release(k)
        if pend_fin[0] is not None:
            pend_fin[0]()
            pend_fin[0] = None
            free_banks[:] = [AB[0], AB[1], AB[2], AB[3], AB[4], AB[5], AB[6], AB[7]]

        if t + 1 < nta:
            rope_tables(t + 1)
        staging_dmas({1: 1, 2: 2, 3: 2, 4: 2, 5: 1}.get(t, 0) if nta == NT else (2 if t < 2 else 1))
        S.stage = "A%d:wout" % t

        def wout_groups(bs):
            ks = {g_: acquire() for g_ in bs}
            for part in ((0, 1, 2), (4, 5, 6, 7), (3,)):
                for (b, nh) in bs:
                    for kc in part:
                        mm(pm[ks[(b, nh)]], mixT[:, kc, b * P:(b + 1) * P], wout[:, kc, nh * 512:(nh + 1) * 512], kc == 0, kc == 3,
                           [("mixT", kc), "wout"], [("pm", ks[(b, nh)])])
            for (b, nh) in bs:
                dst = xt[slot][:, b, nh * 512:(nh + 1) * 512]
                tt("dve", dst, pm[ks[(b, nh)]], dst, ALU.add, [("pm", ks[(b, nh)]), ("xt", slot, b)], [("xt", slot, b)])
                release(ks[(b, nh)])
            for b in sorted({b_ for b_, _ in bs}):
                dma("sp", blk_rows(out_d, t, b), xt[slot][:, b, :], [("xt", slot, b)], [("outd", t, b)])

        if t0n:
            for b in range(4):
                if b >= 1:
                    S.stage = "A1:norm"
                    norm_tr(b - 1)
                    if b + 1 < 4:
                        norm_pre(1, b + 1, gmix_b, "gmix", 0)
                    S.stage = "A%d:wout" % t
                wout_groups([(b, 0), (b, 1)])
        else:
            wout_groups([(0, 0), (0, 1), (1, 0), (1, 1)])
            wout_groups([(2, 0), (2, 1)])
            wout_groups([(3, 0), (3, 1)])
        if t0n:
            S.stage = "A1:norm"
            norm_tr(3)

    S.stage = "B:load"
    barrier = [S.last_on[e] for e in ["pe", "act", "dve", "pool", "sp"]]
    wupv = wupbf_d.rearrange("(c p) n -> p c n", p=P)
    wdnv = wdnbf_d.rearrange("(c p) n -> p c n", p=P)

    def load_wdn(g4):
        dma("sp", wdn[:, g4 * 4:(g4 + 1) * 4, :], wdnv[:, g4 * 4:(g4 + 1) * 4, :], [("wdnbf", g4 // 2)], [("wdn", g4)], extra=barrier)

    for g4 in range(4):
        load_wdn(g4)
    for kc2 in range(4):
        dma("sp", wupB[:, kc2 * 2:kc2 * 2 + 2, :], wupv[:, kc2 * 2:kc2 * 2 + 2, 2048:4096],
            [("wupbf", 2), ("wupbf", 3)], [("wup", 1)], extra=barrier)
    for g4 in range(4, 8):
        load_wdn(g4)

    finals = []
    XNR[0] = [xn[0], xn[1], xnB[0], xnB[1]]

    def up_half(t, half):
        S.stage = "B%d:up%d" % (t, half)
        ex = barrier if (t == 0 and half == 0) else ()
        for f16 in range(16):
            k = acquire()
            for kc in range(8):
                mm(pm[k], wupH[half][:, kc, f16 * P:(f16 + 1) * P], xnT[:, kc, :], kc == 0, kc == 7, [("wup", half)] + XNT_ALL, [("pm", k)])
            r, rn = rr[f16 % 2], ("rr", f16 % 2)
            S.add("act", lambda e, r=r, k=k: e.activation(out=r, in_=pm[k], func=AF.Relu), [("pm", k)], [rn], extra=ex)
            release(k)
            S.add("dve" if f16 % 2 == 0 else "pool", lambda e, r=r, f16=f16: e.tensor_tensor(out=aT[:, f16, :], in0=r, in1=r, op=ALU.mult),
                  [rn], [("aT", f16)], extra=ex)

    def fin_block(t, b):
        slot = t % 2
        xs = xt[slot][:, b, :]
        S.add("act", lambda e: e.activation(out=rr[b % 2][:, 0:512].bitcast(BF16), in_=xs, func=AF.Square, accum_out=ss_t[:, 4 + b:5 + b]),
              [("xt", slot, b)], [("rr", b % 2), ("ss", 4 + b)])
        act(rs_t[:, 4 + b:5 + b], ss_t[:, 4 + b:5 + b], AF.Ln, [("ss", 4 + b), "eps"], [("rs", 4 + b)], scale=1.0 / D, bias=eps_t)
        act(rs_t[:, 4 + b:5 + b], rs_t[:, 4 + b:5 + b], AF.Exp, [("rs", 4 + b)], [("rs", 4 + b)], scale=-0.5)
        stt("dve", xs, xs, rs_t[:, 4 + b:5 + b], gfin_b, ALU.mult, ALU.mult, [("xt", slot, b), ("rs", 4 + b), "gfin"], [("xt", slot, b)])
        finals.append(dma("sp", blk_rows(out_d, t, b), xs, [("xt", slot, b)], [("outd", t, b)]))

    def down_half(t, half, fin=False):
        slot = t % 2
        for b in range(4):
            S.stage = "B%d:down%d" % (t, half)
            for nh in range(2):
                k = acquire()
                for f16 in range(16):
                    f = half * 16 + f16
                    mm(pm[k], aT[:, f16, b * P:(b + 1) * P], wdn[:, f, nh * 512:(nh + 1) * 512], f16 == 0, f16 == 15,
                       [("aT", f16), ("wdn", f // 4)], [("pm", k)])
                dst = xt[slot][:, b, nh * 512:(nh + 1) * 512]
                tt("dve", dst, pm[k], dst, ALU.add, [("pm", k), ("xt", slot, b)], [("xt", slot, b)])
                release(k)
            if fin:
                S.stage = "B%d:fin" % t
                fin_block(t, b)

    for t in range(ntb):
        slot = t % 2
        ring_guard = barrier if t == 0 else ()
        if t + 1 < ntb:
            for b_ in range(4):
                dma("sp", xt[1 - slot][:, b_, :], blk_rows(out_d, t + 1, b_), [("outd", t + 1, b_)], [("xt", 1 - slot, b_)])
        up_half(t, 0)
        down_half(t, 0)
        nxt = t + 1 < ntb
        if nxt:
            S.stage = "B%d:norm" % (t + 1)
            for b_ in range(4):
                norm_pre(1 - slot, b_, gmlp_b, "gmlp", 0, extra=ring_guard)
        up_half(t, 1)
        if nxt:
            S.stage = "B%d:norm" % (t + 1)
            for b_ in range(4):
                norm_tr(b_)
        down_half(t, 1, fin=True)
    S.emit(nc, final_waits=finals)
    return nc


_CACHE = {}


def _consts():
    inv = (np.float32(1.0) / np.power(np.float32(10000.0), np.arange(0, ROPE, 2, dtype=np.float32) / np.float32(ROPE))).astype(np.float32)
    ropec = np.zeros((P, 4), np.float32)
    for p in range(P):
        sgn = -1.0 if (p % 64) < 32 else 1.0
        ropec[p] = (inv[p % 32] / (2 * np.pi), sgn * 2 * np.pi, -sgn * np.pi, -np.pi)
    pc = np.ones((P, 4, 16), np.float32)
    for g in range(4):
        w = 2 << g
        for tt_ in range(w - 1):
            pc[:, g, tt_] = w / (tt_ + 1.0)
    return ropec, pc.reshape(P, 64)


def kernel(**inputs):
    x = np.asarray(inputs["x"], np.float32)
    pos = np.asarray(inputs["positions"], np.int32)
    if "nc" not in _CACHE:
        _CACHE["nc"] = build_program()
    nc = _CACHE["nc"]
    ropec, pc = _consts()
    wnames = ["norm_mix_g", "w_in", "q_norm_g", "w_q_up", "kv_norm_g", "w_kv_up", "pool_w", "pool_scale",
              "w_out", "norm_mlp_g", "w_mlp_up", "w_mlp_down", "norm_final_g"]
    shared = {n: np.ascontiguousarray(np.asarray(inputs[n], np.float32)) for n in wnames}
    in_maps = []
    for i in range(NCORES):
        m = dict(shared)
        m["x"] = np.ascontiguousarray(x[2 * i:2 * i + 2].reshape(TOK, D))
        m["positions"] = np.ascontiguousarray(pos[2 * i:2 * i + 2].reshape(TOK))
        m["ropec"] = ropec
        m["poolcorr"] = pc
        in_maps.append(m)
    res = run_bass_kernel_spmd(nc, in_maps, core_ids=list(range(NCORES)))
    out = np.concatenate([np.asarray(r["out"]).reshape(2, SEQ, D) for r in res.results], axis=0)
    return out.astype(np.float32)
```
